# Optimizing a Trainium2 kernel written in Bass

```python
import math
import jax, jax.numpy as jnp
from jax import lax
import numpy as np

D_MODEL = 1024
BATCH = 2
SEQ = 8192
DEPTH = 2

SSM_WIDTH = 512
SSM_GROUP = 16
SSM_GROUPS = SSM_WIDTH // SSM_GROUP
SSM_STATE = 64
LOG_DT_MIN = math.log(1e-3)
LOG_DT_MAX = math.log(1e-1)
ATTN_HEADS = 8
HEAD_DIM = 64
ATTN_WIDTH = ATTN_HEADS * HEAD_DIM
Q_BLOCK = 128
D_FF = 2816
FFN_RES = 0.5
N_SUB = 3
RMS_EPS = 1e-6
IN_WIDTH = SSM_WIDTH + 3 * ATTN_WIDTH + ATTN_HEADS + 2 * D_MODEL

kernel_name = "hybrid_s5_fox_macaron_adaln"


def rms_norm(x, g):
    xf = x.astype(jnp.float32)
    xf = xf * lax.rsqrt(jnp.mean(xf * xf, axis=-1, keepdims=True) + RMS_EPS)
    return (xf * g.astype(jnp.float32)).astype(x.dtype)


def swiglu(h, w_in, w_out):
    gate, up = jnp.split(h @ w_in, 2, axis=-1)
    return (jax.nn.silu(gate) * up) @ w_out


def s5_ssm(u, a_re, a_im, log_dt, b_re, b_im, c_re, c_im, d_skip):
    f32 = jnp.float32
    bsz, L, _ = u.shape
    uf = u.astype(f32).reshape(bsz, L, SSM_GROUPS, SSM_GROUP)
    lam = lax.complex(jnp.minimum(a_re.astype(f32), -1e-4), a_im.astype(f32))
    dt = jnp.exp(log_dt.astype(f32))[:, None]
    lam_bar = jnp.exp(lam * dt)
    b = lax.complex(b_re.astype(f32), b_im.astype(f32))
    b_bar = ((lam_bar - 1.0) / lam)[..., None] * b
    bu = jnp.einsum('blgn,gpn->blgp', uf.astype(jnp.complex64), b_bar)
    a_all = jnp.broadcast_to(lam_bar, bu.shape)

    def combine(e1, e2):
        a1, s1 = e1
        a2, s2 = e2
        return a2 * a1, a2 * s1 + s2

    _, states = lax.associative_scan(combine, (a_all, bu), axis=1)
    c = lax.complex(c_re.astype(f32), c_im.astype(f32))
    y = jnp.real(jnp.einsum('gnp,blgp->blgn', c, states))
    y = y + d_skip.astype(f32).reshape(SSM_GROUPS, SSM_GROUP) * uf
    return y.reshape(bsz, L, SSM_WIDTH).astype(u.dtype)


def forgetting_attention(q, k, v, f_logit):
    f32 = jnp.float32
    bsz, L, H, Dh = q.shape
    nb = L // Q_BLOCK
    log_f = jax.nn.log_sigmoid(f_logit.astype(f32))
    cum = jnp.cumsum(log_f, axis=1).transpose(0, 2, 1)
    kt = k.transpose(0, 2, 1, 3)
    vt = v.transpose(0, 2, 1, 3)
    qb = q.transpose(0, 2, 1, 3).reshape(bsz, H, nb, Q_BLOCK, Dh).transpose(2, 0, 1, 3, 4)
    cqb = cum.reshape(bsz, H, nb, Q_BLOCK).transpose(2, 0, 1, 3)
    starts = jnp.arange(nb, dtype=jnp.int32) * Q_BLOCK
    k_pos = jnp.arange(L, dtype=jnp.int32)
    scale = Dh ** -0.5

    def one_block(args):
        q_blk, cq_blk, start = args
        s = jnp.einsum('bhqd,bhkd->bhqk', q_blk, kt).astype(f32) * scale
        s = s + (cq_blk[..., None] - cum[:, :, None, :])
        q_pos = start + jnp.arange(Q_BLOCK, dtype=jnp.int32)
        s = jnp.where(k_pos[None, :] <= q_pos[:, None], s, -jnp.inf)
        p = jax.nn.softmax(s, axis=-1).astype(vt.dtype)
        return jnp.einsum('bhqk,bhkd->bhqd', p, vt)

    out = lax.map(one_block, (qb, cqb, starts))
    return out.transpose(1, 0, 3, 2, 4).reshape(bsz, L, H * Dh)


def token_mixer(h, w_in, forget_b, a_re, a_im, log_dt, b_re, b_im, c_re, c_im,
                d_skip, glu_w, attn_w_out, w_out):
    bsz, L, _ = h.shape
    proj = h @ w_in
    cuts = np.cumsum([SSM_WIDTH, ATTN_WIDTH, ATTN_WIDTH, ATTN_WIDTH, ATTN_HEADS, D_MODEL]).tolist()
    u, q, k, v, f, g_a, g_b = jnp.split(proj, cuts, axis=-1)
    y_ssm = s5_ssm(u, a_re, a_im, log_dt, b_re, b_im, c_re, c_im, d_skip)
    z_val, z_gate = jnp.split(jax.nn.gelu(y_ssm) @ glu_w, 2, axis=-1)
    y_a = z_val * jax.nn.sigmoid(z_gate)
    shp = (bsz, L, ATTN_HEADS, HEAD_DIM)
    attn = forgetting_attention(q.reshape(shp), k.reshape(shp), v.reshape(shp), f + forget_b)
    y_b = attn @ attn_w_out
    merged = jax.nn.sigmoid(g_a) * y_a + jax.nn.sigmoid(g_b) * y_b
    return merged @ w_out


def setup_inputs(seed: int = 0) -> dict:
    key = jax.random.key(seed)
    ks = jax.random.split(key, 24)
    f32 = jnp.float32
    nrm = lambda k, shape, s: jax.random.normal(k, shape, f32) * s
    D, G, P, N = D_MODEL, SSM_GROUPS, SSM_STATE, SSM_GROUP
    a_im_init = jnp.pi * jnp.arange(P, dtype=f32)
    return {
        "x": nrm(ks[0], (BATCH, SEQ, D), 1.0),
        "c": nrm(ks[1], (BATCH, D), 1.0),
        "mod_w": nrm(ks[2], (DEPTH, D, N_SUB * 3 * D), 0.5 * D ** -0.5),
        "mod_b": nrm(ks[3], (DEPTH, N_SUB * 3 * D), 0.01),
        "norm_pre": 1.0 + nrm(ks[4], (DEPTH, N_SUB, D), 0.02),
        "norm_post": 1.0 + nrm(ks[5], (DEPTH, N_SUB, D), 0.02),
        "ffn_w_in": nrm(ks[6], (DEPTH, 2, D, 2 * D_FF), D ** -0.5),
        "ffn_w_out": nrm(ks[7], (DEPTH, 2, D_FF, D), D_FF ** -0.5),
        "mix_w_in": nrm(ks[8], (DEPTH, D, IN_WIDTH), D ** -0.5),
        "forget_b": 3.0 + nrm(ks[9], (DEPTH, ATTN_HEADS), 0.5),
        "ssm_a_re": -0.5 + nrm(ks[10], (DEPTH, G, P), 0.01),
        "ssm_a_im": a_im_init + nrm(ks[11], (DEPTH, G, P), 0.01),
        "ssm_log_dt": jax.random.uniform(ks[12], (DEPTH, G), f32, LOG_DT_MIN, LOG_DT_MAX),
        "ssm_b_re": nrm(ks[13], (DEPTH, G, P, N), (2 * N) ** -0.5),
        "ssm_b_im": nrm(ks[14], (DEPTH, G, P, N), (2 * N) ** -0.5),
        "ssm_c_re": nrm(ks[15], (DEPTH, G, N, P), (2 * P) ** -0.5),
        "ssm_c_im": nrm(ks[16], (DEPTH, G, N, P), (2 * P) ** -0.5),
        "ssm_d": nrm(ks[17], (DEPTH, SSM_WIDTH), 1.0),
        "glu_w": nrm(ks[18], (DEPTH, SSM_WIDTH, 2 * D), SSM_WIDTH ** -0.5),
        "attn_w_out": nrm(ks[19], (DEPTH, ATTN_WIDTH, D), ATTN_WIDTH ** -0.5),
        "mix_w_out": nrm(ks[20], (DEPTH, D, D), D ** -0.5),
    }


def reference(x, c, mod_w, mod_b, norm_pre, norm_post, ffn_w_in, ffn_w_out,
              mix_w_in, forget_b, ssm_a_re, ssm_a_im, ssm_log_dt, ssm_b_re,
              ssm_b_im, ssm_c_re, ssm_c_im, ssm_d, glu_w, attn_w_out, mix_w_out):
    bsz = x.shape[0]
    for l in range(DEPTH):
        mod = (jax.nn.silu(c) @ mod_w[l] + mod_b[l]).reshape(bsz, N_SUB, 3, D_MODEL)
        mod = mod[:, :, :, None, :]

        def pre(x_in, i):
            return rms_norm(x_in, norm_pre[l, i]) * (1.0 + mod[:, i, 1]) + mod[:, i, 0]

        def post_add(x_in, y, i, res_w):
            return x_in + res_w * mod[:, i, 2] * rms_norm(y, norm_post[l, i])

        x = post_add(x, swiglu(pre(x, 0), ffn_w_in[l, 0], ffn_w_out[l, 0]), 0, FFN_RES)
        y = token_mixer(pre(x, 1), mix_w_in[l], forget_b[l], ssm_a_re[l], ssm_a_im[l],
                        ssm_log_dt[l], ssm_b_re[l], ssm_b_im[l], ssm_c_re[l], ssm_c_im[l],
                        ssm_d[l], glu_w[l], attn_w_out[l], mix_w_out[l])
        x = post_add(x, y, 1, 1.0)
        x = post_add(x, swiglu(pre(x, 2), ffn_w_in[l, 1], ffn_w_out[l, 1]), 2, FFN_RES)
    return x
```

```python
import contextlib
import math

import numpy as np
import concourse.bass as bass
import concourse.mybir as mybir
from concourse.bass_utils import run_bass_kernel_spmd

F32 = mybir.dt.float32
BF16 = mybir.dt.bfloat16
AF = mybir.ActivationFunctionType
ALU = mybir.AluOpType
AX = mybir.AxisListType

D_MODEL = 1024
BATCH = 2
SEQ = 8192
DEPTH = 2
D_FF = 2816
NCORES = 8


class Buf:
    __slots__ = ("name", "last_w", "readers")

    def __init__(self, name):
        self.name = name
        self.last_w = None
        self.readers = []


class Op:
    __slots__ = ("eng", "fn", "waits", "idx", "needs_inc", "dma_sem", "dma_val", "is_dma")

    def __init__(self, eng, fn):
        self.eng = eng
        self.fn = fn
        self.waits = []
        self.idx = -1
        self.needs_inc = False
        self.dma_sem = None
        self.dma_val = 0
        self.is_dma = False


ENGS = ("pe", "act", "dve", "pool", "sp")


class Prog:
    def __init__(self, nc, stack, n_dma_sems=40):
        self.nc = nc
        self.stack = stack
        self.ops = {e: [] for e in ENGS}
        self.sem = {e: stack.enter_context(nc.semaphore("sem_" + e)) for e in ENGS}
        self.dma_sems = [stack.enter_context(nc.semaphore("dsem%d" % i)) for i in range(n_dma_sems)]
        self.dma_use = [0] * n_dma_sems
        self.dma_last = [None] * n_dma_sems
        self.dma_rr = 0
        self.known = {e: {} for e in ENGS}
        self.out_dmas = []
        self.nbuf = 0

    def buf(self, name=None):
        self.nbuf += 1
        return Buf(name or ("b%d" % self.nbuf))

    def bufs(self, n, name="b"):
        return [self.buf("%s%d" % (name, i)) for i in range(n)]

    def _key_val(self, ev):
        if ev.is_dma:
            return ("d", ev.dma_sem), ev.dma_val
        return ev.eng, ev.idx

    def _add_wait(self, op, ev):
        if ev is None or ev is op:
            return
        if (not ev.is_dma) and ev.eng == "pe" and op.eng == "pe" and not op.is_dma:
            return
        key, val = self._key_val(ev)
        kn = self.known[op.eng]
        if kn.get(key, -1) >= val:
            return
        kn[key] = val
        op.waits.append(ev)
        if not ev.is_dma:
            ev.needs_inc = True

    def _deps(self, op, reads, writes):
        for b in reads:
            self._add_wait(op, b.last_w)
        for b in writes:
            self._add_wait(op, b.last_w)
            for r in b.readers:
                self._add_wait(op, r)
        for b in reads:
            b.readers.append(op)
        for b in writes:
            b.last_w = op
            b.readers = []

    def op(self, eng, fn, reads=(), writes=()):
        o = Op(eng, fn)
        o.idx = len(self.ops[eng])
        self._deps(o, reads, writes)
        self.ops[eng].append(o)
        return o

    def dma(self, fn, reads=(), writes=(), queue="sp", is_out=False):
        o = Op(queue, fn)
        o.is_dma = True
        o.idx = len(self.ops[queue])
        k = self.dma_rr
        self.dma_rr = (self.dma_rr + 1) % len(self.dma_sems)
        prev = self.dma_last[k]
        if prev is not None:
            key, val = self._key_val(prev)
            kn = self.known[queue]
            if kn.get(key, -1) < val:
                kn[key] = val
                o.waits.append(prev)
        self.dma_use[k] += 1
        o.dma_sem = k
        o.dma_val = 16 * self.dma_use[k]
        self.dma_last[k] = o
        self._deps(o, reads, writes)
        self.ops[queue].append(o)
        if is_out:
            self.out_dmas.append(o)
        return o

    def emit(self):
        nc = self.nc
        fin = Op("sp", None)
        fin.idx = len(self.ops["sp"])
        for o in self.out_dmas:
            self._add_wait(fin, o)
        for e in ENGS:
            if e != "sp" and self.ops[e]:
                self._add_wait(fin, self.ops[e][-1])
        self.ops["sp"].append(fin)
        rank = {}
        for e in ENGS:
            r = 0
            for o in self.ops[e]:
                if o.needs_inc and not o.is_dma:
                    r += 1
                    rank[id(o)] = r

        def run(eng_name, eng):
            for o in self.ops[eng_name]:
                for w in o.waits:
                    if w.is_dma:
                        eng.wait_ge(self.dma_sems[w.dma_sem], w.dma_val)
                    else:
                        eng.wait_ge(self.sem[w.eng], rank[id(w)])
                if o.fn is None:
                    continue
                ins = o.fn(eng)
                if o.is_dma:
                    ins.then_inc(self.dma_sems[o.dma_sem], 16)
                elif o.needs_inc:
                    ins.then_inc(self.sem[eng_name], 1)

        with nc.Block() as block:
            @block.sync
            def _(e):
                run("sp", e)

            @block.tensor
            def _(e):
                run("pe", e)

            @block.scalar
            def _(e):
                run("act", e)

            @block.vector
            def _(e):
                run("dve", e)

            @block.gpsimd
            def _(e):
                run("pool", e)


class V:
    __slots__ = ("ap", "bufs")

    def __init__(self, ap, bufs):
        self.ap = ap
        self.bufs = list(bufs) if isinstance(bufs, (list, tuple)) else [bufs]


def _b(*vs):
    out = []
    for v in vs:
        if isinstance(v, V):
            out.extend(v.bufs)
    return out


def _a(v):
    return v.ap if isinstance(v, V) else v


class K:
    def __init__(self, P):
        self.P = P

    def mm(self, out, lhsT, rhs, start, stop, extra_reads=()):
        o, l, r = out.ap, lhsT.ap, rhs.ap
        return self.P.op("pe", lambda e: e.matmul(o, lhsT=l, rhs=r, start=start, stop=stop, skip_group_check=True),
                         reads=_b(lhsT, rhs) + list(extra_reads), writes=_b(out))

    def transpose(self, out, in_, ident):
        o, i, d = out.ap, in_.ap, ident.ap
        return self.P.op("pe", lambda e: e.transpose(o, i, d), reads=_b(in_, ident), writes=_b(out))

    def act(self, out, in_, func, scale=1.0, bias=0.0, eng="act"):
        o, i, s, b = out.ap, in_.ap, _a(scale), _a(bias)
        return self.P.op("act", lambda e: e.activation(out=o, in_=i, func=func, bias=b, scale=s),
                         reads=_b(in_, scale, bias), writes=_b(out))

    def tt(self, out, in0, in1, op, eng="dve"):
        o, a, b = out.ap, in0.ap, in1.ap
        return self.P.op(eng, lambda e: e.tensor_tensor(out=o, in0=a, in1=b, op=op), reads=_b(in0, in1), writes=_b(out))

    def ts(self, out, in0, s1, op0, s2=None, op1=None, eng="dve"):
        o, a, x1, x2 = out.ap, in0.ap, _a(s1), _a(s2)
        if op1 is None:
            f = lambda e: e.tensor_scalar(out=o, in0=a, scalar1=x1, scalar2=None, op0=op0)
        else:
            f = lambda e: e.tensor_scalar(out=o, in0=a, scalar1=x1, scalar2=x2, op0=op0, op1=op1)
        return self.P.op(eng, f, reads=_b(in0, s1, s2), writes=_b(out))

    def stt(self, out, in0, scalar, in1, op0, op1):
        o, a, s, b = out.ap, in0.ap, _a(scalar), in1.ap
        return self.P.op("dve", lambda e: e.scalar_tensor_tensor(out=o, in0=a, scalar=s, in1=b, op0=op0, op1=op1),
                         reads=_b(in0, scalar, in1), writes=_b(out))

    def copy(self, out, in_, eng="dve"):
        o, i = out.ap, in_.ap
        return self.P.op(eng, lambda e: e.tensor_copy(out=o, in_=i), reads=_b(in_), writes=_b(out))

    def recip(self, out, in_):
        o, i = out.ap, in_.ap
        return self.P.op("dve", lambda e: e.reciprocal(out=o, in_=i), reads=_b(in_), writes=_b(out))

    def memset(self, out, val, eng="dve"):
        o = out.ap
        return self.P.op(eng, lambda e: e.memset(o, val), writes=_b(out))

    def scan(self, out, d0, d1, init, op0=ALU.mult, op1=ALU.add):
        o, a, b, i = out.ap, d0.ap, d1.ap, _a(init)
        return self.P.op("dve", lambda e: e.tensor_tensor_scan(out=o, data0=a, data1=b, initial=i, op0=op0, op1=op1),
                         reads=_b(d0, d1, init), writes=_b(out))

    def dma(self, out, in_, queue="sp", is_out=False, extra_reads=(), extra_writes=()):
        o, i = out.ap, in_.ap
        return self.P.dma(lambda e: e.dma_start(out=o, in_=i), reads=_b(in_) + list(extra_reads),
                          writes=_b(out) + list(extra_writes), queue=queue, is_out=is_out)


class Rot:
    def __init__(self, P, aps):
        self.items = [V(ap, P.buf()) for ap in aps]
        self.i = 0

    def get(self):
        v = self.items[self.i]
        self.i = (self.i + 1) % len(self.items)
        return v


def sub(v, ap):
    return V(ap, v.bufs)


TOK = 2048
PASS = 1024
BLK = 512
IN_W = 4104
GELU_C = 1.5957691216057308


def build_tp(stage):
    has_C = stage in ("CA", "C1")
    has_A = stage in ("A0", "CA")
    lc = {"CA": 0, "C1": 1}.get(stage)
    la = {"A0": 0, "CA": 1}.get(stage)
    layers = sorted(set(l for l in (lc, la) if l is not None))
    nc = bass.Bass("TRN2", target_bir_lowering=False)
    dram_in = lambda name, shape: nc.dram_tensor(name, shape, F32, kind="ExternalInput").ap()
    dram_out = lambda name, shape: nc.dram_tensor(name, shape, F32, kind="ExternalOutput").ap()
    xin = dram_in("xin", [1024, TOK])
    xout = dram_out("xout", [1024, TOK])
    cT_d = dram_in("cT", [128, 8])
    par = {}
    for l in layers:
        par[l] = dict(mod_w=dram_in("mod_w%d" % l, [1024, 9216]), mod_bT=dram_in("mod_bT%d" % l, [128, 72]),
                      npre=dram_in("npre%d" % l, [128, 24]), npost=dram_in("npost%d" % l, [128, 24]))
    if has_C:
        par[lc].update(glu_w=dram_in("glu_w", [512, 2048]), attn_w=dram_in("attn_w", [512, 1024]),
                       mixo=dram_in("mixo", [1024, 1024]), w_in=dram_in("c_w_in", [1024, 5632]),
                       w_out=dram_in("c_w_out", [2816, 1024]))
        yssm_d = dram_in("yssmT", [512, TOK])
        attn_d = dram_in("attnT", [512, TOK])
        sga_i = dram_in("sgaT_in", [1024, TOK])
        sgb_i = dram_in("sgbT_in", [1024, TOK])
    if has_A:
        par[la].update(a_w_in=dram_in("a_w_in", [1024, 5632]), a_w_out=dram_in("a_w_out", [2816, 1024]),
                       mix_in=dram_in("mix_in", [1024, IN_W]))
        uT_o = dram_out("uT", [512, TOK])
        qT_o = dram_out("qT", [512, TOK])
        kT_o = dram_out("kT", [512, TOK])
        v_o = dram_out("vtok", [TOK, 512])
        fT_o = dram_out("fT", [8, TOK])
        sga_o = dram_out("sgaT", [1024, TOK])
        sgb_o = dram_out("sgbT", [1024, TOK])

    with contextlib.ExitStack() as st:
        P = Prog(nc, st)
        k_ = K(P)
        sb = lambda name, shape, dt: st.enter_context(nc.sbuf_tensor(name, shape, dt))
        xT = sb("xT", [128, 8, PASS], F32)
        hT = sb("hT", [128, 8, PASS], BF16)
        aT = sb("aT", [128, 22, PASS], BF16)
        yT = sb("yT", [128, 2, 8, BLK], F32)
        win_t = sb("win", [128, 3, 8, 2, 128], BF16)
        wout_t = sb("wout", [128, 2, 22, 128], BF16)
        wv_t = sb("wv", [128, 8, 512], BF16)
        sq_t = sb("sq", [128, 2, BLK], F32)
        rstd_t = sb("rstd", [128, 2, BLK], F32)
        tmp_t = sb("tmp", [128, 8, BLK], F32)
        cph_t = sb("cph", [128, 16, BLK], BF16)
        ones_t = sb("ones", [128, 128], F32)
        small_t = sb("small", [128, 512], F32)
        scb_t = sb("scb", [128, 8], BF16)
        ps_t = [st.enter_context(nc.psum_tensor("ps%d" % i, [128, 512], F32)) for i in range(8)]

        xB = [[P.buf() for _ in range(2)] for _ in range(8)]
        hB = [[P.buf() for _ in range(2)] for _ in range(8)]
        aB = [[P.buf() for _ in range(2)] for _ in range(22)]
        yB = [[P.buf() for _ in range(8)] for _ in range(2)]
        xV = lambda k, blk: V(xT[:, k, blk * BLK:(blk + 1) * BLK], xB[k][blk])
        hV = lambda k, blk: V(hT[:, k, blk * BLK:(blk + 1) * BLK], hB[k][blk])
        aV = lambda j, blk: V(aT[:, j, blk * BLK:(blk + 1) * BLK], aB[j][blk])
        yV = lambda blk, k: V(yT[:, blk, k, :], yB[blk][k])
        win = Rot(P, [win_t[:, i] for i in range(3)])
        wout = Rot(P, [wout_t[:, i] for i in range(2)])
        wv = V(wv_t[:], P.buf())
        sq = Rot(P, [sq_t[:, i, :] for i in range(2)])
        rstd = [V(rstd_t[:, i, :], P.buf()) for i in range(2)]
        tmp = Rot(P, [tmp_t[:, i, :] for i in range(8)])
        ps = Rot(P, [t[:] for t in ps_t])
        ones = V(ones_t[:], P.buf())
        k_.memset(ones, 1.0)
        aflat = aT[:].rearrange("p a b -> p (a b)")

        cst = {}
        for li, l in enumerate(layers):
            modT = sb("modT%d" % l, [128, 72], F32)
            gs = sb("gs%d" % l, [128, 24], F32)
            coef = sb("coef%d" % l, [128, 24], F32)
            npre = sb("npre_s%d" % l, [128, 24], F32)
            npost = sb("npost_s%d" % l, [128, 24], F32)
            modb = sb("modb_s%d" % l, [128, 72], F32)
            cst[l] = dict(modT=V(modT[:], P.buf()), gs=V(gs[:], P.buf()), coef=V(coef[:], P.buf()),
                          npre=V(npre[:], P.buf()), npost=V(npost[:], P.buf()), modb=V(modb[:], P.buf()))
            k_.dma(cst[l]["npre"], V(par[l]["npre"], []))
            k_.dma(cst[l]["npost"], V(par[l]["npost"], []))
            k_.dma(cst[l]["modb"], V(par[l]["mod_bT"], []))

        cs = V(small_t[:, 0:8], P.buf())
        k_.dma(cs, V(cT_d, []))
        scb = V(scb_t[:], P.buf())
        k_.act(scb, cs, AF.Silu)
        for l in layers:
            mw = par[l]["mod_w"].rearrange("(k p) c -> p k c", p=128)
            mps = ps.get()
            for ch in range(18):
                s = ch % 5
                slot = V(aflat[:, s * 4096:(s + 1) * 4096].rearrange("p (k c) -> p k c", k=8),
                         [aB[j][b] for j in range(4 * s, 4 * s + 4) for b in range(2)])
                k_.dma(slot, V(mw[:, :, ch * 512:(ch + 1) * 512], []), queue="pool")
                for ct in range(4):
                    col = ch * 4 + ct
                    for k in range(8):
                        k_.mm(sub(mps, mps.ap[:, col:col + 1]), sub(slot, slot.ap[:, k, ct * 128:(ct + 1) * 128]),
                              sub(scb, scb.ap[:, k:k + 1]), start=(k == 0), stop=(k == 7))
            c = cst[l]
            k_.tt(c["modT"], sub(mps, mps.ap[:, 0:72]), c["modb"], ALU.add)
            for s_ in range(3):
                m = c["modT"].ap
                t8 = V(small_t[:, 16:24], P.buf())
                k_.ts(t8, sub(c["modT"], m[:, s_ * 24 + 8:s_ * 24 + 16]), 1.0, ALU.add)
                k_.tt(sub(c["gs"], c["gs"].ap[:, s_ * 8:(s_ + 1) * 8]), t8, sub(c["npre"], c["npre"].ap[:, s_ * 8:(s_ + 1) * 8]), ALU.mult)
                res_w = 1.0 if s_ == 1 else 0.5
                k_.stt(sub(c["coef"], c["coef"].ap[:, s_ * 8:(s_ + 1) * 8]), sub(c["modT"], m[:, s_ * 24 + 16:s_ * 24 + 24]),
                       res_w, sub(c["npost"], c["npost"].ap[:, s_ * 8:(s_ + 1) * 8]), ALU.mult, ALU.mult)

        def rms_rstd(src_of_k, blk):
            acc = ps.get()
            for k in range(8):
                s = sq.get()
                k_.act(s, src_of_k(k), AF.Square)
                k_.mm(acc, ones, s, start=(k == 0), stop=(k == 7))
            t = tmp.get()
            k_.act(t, acc, AF.Sqrt, scale=1.0 / D_MODEL, bias=eps_v)
            k_.recip(rstd[blk], t)

        def prenorm(l, s_):
            c = cst[l]
            for blk in range(2):
                rms_rstd(lambda k: xV(k, blk), blk)
                for k in range(8):
                    t = tmp.get()
                    k_.tt(t, xV(k, blk), rstd[blk], ALU.mult)
                    col = s_ * 8 + k
                    k_.act(hV(k, blk), t, AF.Identity, scale=sub(c["gs"], c["gs"].ap[:, col:col + 1]),
                           bias=sub(c["modT"], c["modT"].ap[:, s_ * 24 + k:s_ * 24 + k + 1]))

        def post_add(l, s_):
            c = cst[l]
            for blk in range(2):
                rms_rstd(lambda k: yV(blk, k), blk)
                for k in range(8):
                    t = tmp.get()
                    k_.tt(t, yV(blk, k), rstd[blk], ALU.mult)
                    col = s_ * 8 + k
                    k_.stt(xV(k, blk), t, sub(c["coef"], c["coef"].ap[:, col:col + 1]), xV(k, blk), ALU.mult, ALU.add)

        def ffn(w_in_d, w_out_d):
            wi = w_in_d.rearrange("(k p) c -> p k c", p=128)
            wo = w_out_d.rearrange("(f p) c -> p f c", p=128)
            for j in range(22):
                w = win.get()
                k_.dma(sub(w, w.ap[:, :, 0, :]), V(wi[:, :, j * 128:(j + 1) * 128], []), queue="pool")
                k_.dma(sub(w, w.ap[:, :, 1, :]), V(wi[:, :, D_FF + j * 128:D_FF + (j + 1) * 128], []), queue="pool")
                for blk in range(2):
                    G = ps.get()
                    for k in range(8):
                        k_.mm(G, sub(w, w.ap[:, k, 0, :]), hV(k, blk), start=(k == 0), stop=(k == 7))
                    U = ps.get()
                    for k in range(8):
                        k_.mm(U, sub(w, w.ap[:, k, 1, :]), hV(k, blk), start=(k == 0), stop=(k == 7))
                    t = tmp.get()
                    k_.act(t, G, AF.Silu)
                    k_.tt(aV(j, blk), t, U, ALU.mult)
            for dt in range(8):
                w = wout.get()
                k_.dma(w, V(wo[:, :, dt * 128:(dt + 1) * 128], []), queue="pool")
                for blk in range(2):
                    Y = ps.get()
                    for f in range(22):
                        k_.mm(Y, sub(w, w.ap[:, f, :]), aV(f, blk), start=(f == 0), stop=(f == 21))
                    k_.act(yV(blk, dt), Y, AF.Copy)

        def premix(l, tok0):
            mi = par[l]["mix_in"].rearrange("(k p) c -> p k c", p=128)
            tiles = []
            for i in range(4):
                tiles.append((i * 128, 128, uT_o, i * 128, AF.Copy))
            for i in range(4):
                tiles.append((512 + i * 128, 128, qT_o, i * 128, AF.Copy))
            for i in range(4):
                tiles.append((1024 + i * 128, 128, kT_o, i * 128, AF.Copy))
            tiles.append((2048, 8, fT_o, 0, AF.Copy))
            for i in range(8):
                tiles.append((2056 + i * 128, 128, sga_o, i * 128, AF.Sigmoid))
            for i in range(8):
                tiles.append((3080 + i * 128, 128, sgb_o, i * 128, AF.Sigmoid))
            for (c0, n, dst, r0, func) in tiles:
                w = win.get()
                k_.dma(sub(w, w.ap[:, :, 0, 0:n]), V(mi[:, :, c0:c0 + n], []), queue="pool")
                for blk in range(2):
                    O = ps.get()
                    for k in range(8):
                        k_.mm(sub(O, O.ap[0:n, :]), sub(w, w.ap[:, k, 0, 0:n]), hV(k, blk), start=(k == 0), stop=(k == 7))
                    t = tmp.get()
                    k_.act(sub(t, t.ap[0:n, :]), sub(O, O.ap[0:n, :]), func)
                    k_.dma(V(dst[r0:r0 + n, tok0 + blk * BLK:tok0 + (blk + 1) * BLK], []), sub(t, t.ap[0:n, :]), is_out=True)
            k_.dma(wv, V(mi[:, :, 1536:2048], []), queue="pool")
            for tt_ in range(8):
                O = ps.get()
                blk, off = divmod(tt_ * 128, BLK)
                for k in range(8):
                    k_.mm(O, sub(hV(k, blk), hT[:, k, tt_ * 128:(tt_ + 1) * 128]), sub(wv, wv.ap[:, k, :]), start=(k == 0), stop=(k == 7))
                t = tmp.get()
                k_.copy(t, O)
                k_.dma(V(v_o[tok0 + tt_ * 128:tok0 + (tt_ + 1) * 128, :], []), t, is_out=True)

        def cphase(l, tok0):
            p_ = par[l]
            glu_bufs = [aB[j][b] for j in range(0, 8) for b in range(2)]
            att_bufs = [aB[j][b] for j in range(8, 12) for b in range(2)]
            mxo_bufs = [aB[j][b] for j in range(12, 20) for b in range(2)]
            glu = V(aflat[:, 0:8192].rearrange("p (c n) -> p c n", c=4), glu_bufs)
            att = V(aflat[:, 8192:12288].rearrange("p (c n) -> p c n", c=4), att_bufs)
            mxo = V(aflat[:, 12288:20480].rearrange("p (c n) -> p c n", c=8), mxo_bufs)
            k_.dma(glu, V(p_["glu_w"].rearrange("(c p) n -> p c n", p=128), []), queue="pool")
            k_.dma(att, V(p_["attn_w"].rearrange("(c p) n -> p c n", p=128), []), queue="pool")
            k_.dma(mxo, V(p_["mixo"].rearrange("(c p) n -> p c n", p=128), []), queue="pool")
            gl = [V(cph_t[:, i, :], P.buf()) for i in range(4)]
            ab = [V(cph_t[:, 4 + i, :], P.buf()) for i in range(4)]
            mg = [V(cph_t[:, 8 + i, :], P.buf()) for i in range(8)]
            for blk in range(2):
                ts0 = tok0 + blk * BLK
                for ct in range(4):
                    ys = tmp.get()
                    k_.dma(ys, V(yssm_d[ct * 128:(ct + 1) * 128, ts0:ts0 + BLK], []))
                    t1 = tmp.get()
                    k_.tt(t1, ys, ys, ALU.mult)
                    k_.ts(t1, t1, 0.044715, ALU.mult, 1.0, ALU.add)
                    k_.tt(t1, t1, ys, ALU.mult)
                    k_.act(t1, t1, AF.Sigmoid, scale=GELU_C)
                    k_.tt(gl[ct], ys, t1, ALU.mult)
                    k_.dma(ab[ct], V(attn_d[ct * 128:(ct + 1) * 128, ts0:ts0 + BLK], []), queue="pool")
                for dt in range(8):
                    ZV = ps.get()
                    for ct in range(4):
                        k_.mm(ZV, sub(glu, glu.ap[:, ct, dt * 128:(dt + 1) * 128]), gl[ct], start=(ct == 0), stop=(ct == 3))
                    ZG = ps.get()
                    for ct in range(4):
                        k_.mm(ZG, sub(glu, glu.ap[:, ct, 1024 + dt * 128:1024 + (dt + 1) * 128]), gl[ct], start=(ct == 0), stop=(ct == 3))
                    YB = ps.get()
                    for ct in range(4):
                        k_.mm(YB, sub(att, att.ap[:, ct, dt * 128:(dt + 1) * 128]), ab[ct], start=(ct == 0), stop=(ct == 3))
                    sg = tmp.get()
                    k_.act(sg, ZG, AF.Sigmoid)
                    ya = tmp.get()
                    k_.tt(ya, sg, ZV, ALU.mult)
                    ta = tmp.get()
                    k_.dma(ta, V(sga_i[dt * 128:(dt + 1) * 128, ts0:ts0 + BLK], []))
                    tb = tmp.get()
                    k_.dma(tb, V(sgb_i[dt * 128:(dt + 1) * 128, ts0:ts0 + BLK], []))
                    k_.tt(ya, ya, ta, ALU.mult)
                    k_.tt(tb, tb, YB, ALU.mult)
                    k_.tt(mg[dt], ya, tb, ALU.add)
                for d2 in range(8):
                    Y = ps.get()
                    for dt in range(8):
                        k_.mm(Y, sub(mxo, mxo.ap[:, dt, d2 * 128:(d2 + 1) * 128]), mg[dt], start=(dt == 0), stop=(dt == 7))
                    k_.act(yV(blk, d2), Y, AF.Copy)

        eps_t = sb("eps", [128, 1], F32)
        eps_v = V(eps_t[:], P.buf())
        k_.memset(eps_v, 1e-6)

        xin_v = xin.rearrange("(k p) t -> p k t", p=128)
        xout_v = xout.rearrange("(k p) t -> p k t", p=128)
        for ps_i in range(TOK // PASS):
            tok0 = ps_i * PASS
            for blk in range(2):
                for k in range(8):
                    k_.dma(xV(k, blk), V(xin_v[:, k, tok0 + blk * BLK:tok0 + (blk + 1) * BLK], []))
            if has_C:
                cphase(lc, tok0)
                post_add(lc, 1)
                prenorm(lc, 2)
                ffn(par[lc]["w_in"], par[lc]["w_out"])
                post_add(lc, 2)
            if has_A:
                prenorm(la, 0)
                ffn(par[la]["a_w_in"], par[la]["a_w_out"])
                post_add(la, 0)
            for blk in range(2):
                for k in range(8):
                    k_.dma(V(xout_v[:, k, tok0 + blk * BLK:tok0 + (blk + 1) * BLK], []), xV(k, blk), is_out=True)
            if has_A:
                prenorm(la, 1)
                premix(la, tok0)
        P.emit()
    return nc


def _colT(v):
    v = np.asarray(v, np.float32)
    return np.ascontiguousarray(v.reshape(-1, 128).T)


def tp_common_inputs(inp, stage):
    lc = {"CA": 0, "C1": 1}.get(stage)
    la = {"A0": 0, "CA": 1}.get(stage)
    layers = sorted(set(l for l in (lc, la) if l is not None))
    m = {}
    for l in layers:
        m["mod_w%d" % l] = np.ascontiguousarray(inp["mod_w"][l])
        m["mod_bT%d" % l] = _colT(inp["mod_b"][l])
        m["npre%d" % l] = _colT(inp["norm_pre"][l].reshape(-1))
        m["npost%d" % l] = _colT(inp["norm_post"][l].reshape(-1))
    if lc is not None:
        m["glu_w"] = np.ascontiguousarray(inp["glu_w"][lc])
        m["attn_w"] = np.ascontiguousarray(inp["attn_w_out"][lc])
        m["mixo"] = np.ascontiguousarray(inp["mix_w_out"][lc])
        m["c_w_in"] = np.ascontiguousarray(inp["ffn_w_in"][lc, 1])
        m["c_w_out"] = np.ascontiguousarray(inp["ffn_w_out"][lc, 1])
    if la is not None:
        m["a_w_in"] = np.ascontiguousarray(inp["ffn_w_in"][la, 0])
        m["a_w_out"] = np.ascontiguousarray(inp["ffn_w_out"][la, 0])
        m["mix_in"] = np.ascontiguousarray(inp["mix_w_in"][la])
    return m


_NC_CACHE = {}


def get_nc(key, builder):
    if key not in _NC_CACHE:
        _NC_CACHE[key] = builder()
    return _NC_CACHE[key]


I32 = mybir.dt.int32
TS = 512
NCH = SEQ // TS
NEG = -240000.0
TWO_PI_LO = 6.28318
INV_2PI = 1.0 / (2.0 * math.pi)


def build_mix():
    nc = bass.Bass("TRN2", target_bir_lowering=False)
    dram_in = lambda name, shape: nc.dram_tensor(name, shape, F32, kind="ExternalInput").ap()
    dram_out = lambda name, shape: nc.dram_tensor(name, shape, F32, kind="ExternalOutput").ap()
    uT_d = dram_in("uT", [128, SEQ])
    qT_d = dram_in("qT", [128, SEQ])
    kT_d = dram_in("kT", [128, SEQ])
    v_d = dram_in("vtok", [SEQ, 128])
    fT_d = dram_in("fT", [2, SEQ])
    fb_d = dram_in("fb", [2, 1])
    are_d = dram_in("a_re", [128, 4])
    aim_d = dram_in("a_im", [128, 4])
    ldt_d = dram_in("ldt", [128, 4])
    bre_d = dram_in("b_re", [128, 4, 16])
    bim_d = dram_in("b_im", [128, 4, 16])
    cre_d = dram_in("c_re", [128, 4, 16])
    cim_d = dram_in("c_im", [128, 4, 16])
    dsk_d = dram_in("dskip", [128, 1])
    ys_o = dram_out("yssmT", [128, SEQ])
    at_o = dram_out("attn", [SEQ, 128])

    with contextlib.ExitStack() as st:
        P = Prog(nc, st)
        k_ = K(P)
        sb = lambda name, shape, dt: st.enter_context(nc.sbuf_tensor(name, shape, dt))
        UF_t = sb("UF", [128, SEQ], F32)
        UB_t = sb("UB", [128, SEQ], BF16)
        QT_t = sb("QT", [128, 2 * SEQ], BF16)
        KT_t = sb("KT", [128, 2 * SEQ], BF16)
        VA_t = sb("VA", [128, 64, 2, 65], BF16)
        cos_t = sb("cosT", [128, 4, TS + 1], F32)
        sin_t = sb("sinT", [128, 4, TS + 1], F32)
        W_t = sb("W", [128, 6, TS], F32)
        Z_t = sb("Z", [128, 4, TS], BF16)
        L_t = sb("L", [128, 16, 128], BF16)
        yst_t = sb("yst", [128, 2, TS], F32)
        PT_t = sb("PT", [128, 4, 512], BF16)
        ost_t = sb("ost", [128, 2, 4, 64], F32)
        msk_t = sb("msk", [128, 128], BF16)
        idb_t = sb("idb", [128, 128], BF16)
        idf_t = sb("idf", [128, 128], F32)
        E_t = sb("E", [128, 2, 128], F32)
        zo_t = sb("zo", [128, 2, 128], F32)
        sm_t = sb("sm", [128, 128], F32)
        bb_t = sb("bb", [128, 4, 4, 16], F32)
        prm_t = sb("prm", [128, 4, 4, 16], F32)
        ii_t = sb("ii", [128, TS + 1], I32)
        rc_t = sb("rc", [128, 8], F32)
        ps_t = [st.enter_context(nc.psum_tensor("ps%d" % i, [128, 512], F32)) for i in range(8)]
        S_ps = Rot(P, [ps_t[i][:] for i in range(3)])
        O_ps = Rot(P, [ps_t[3 + i][:] for i in range(2)])
        BUre = V(ps_t[5][:], P.buf())
        BUim = V(ps_t[6][:], P.buf())
        Yps = V(ps_t[7][:], P.buf())

        UF = V(UF_t[:], P.buf())
        UB = V(UB_t[:], P.buf())
        QT = V(QT_t[:], P.buf())
        KT = V(KT_t[:], P.buf())
        VA = V(VA_t[:], P.buf())
        smB = {}

        def sm(name, c0, n):
            if name not in smB:
                smB[name] = V(sm_t[:, c0:c0 + n], P.buf())
            return smB[name]

        zeros = V(zo_t[:, 0, :], P.buf())
        ones = V(zo_t[:, 1, :], P.buf())
        k_.memset(zeros, 0.0)
        k_.memset(ones, 1.0)
        msk = V(msk_t[:], P.buf())
        idb = V(idb_t[:], P.buf())
        idf = V(idf_t[:], P.buf())
        za, oa = zeros.ap, ones.ap
        P.op("pool", lambda e: e.affine_select(out=msk_t[:], in_=za, pattern=[[1, 128]], compare_op=ALU.is_ge,
                                               fill=NEG, base=0, channel_multiplier=-1), reads=_b(zeros), writes=_b(msk))
        P.op("pool", lambda e: e.affine_select(out=idb_t[:], in_=oa, pattern=[[1, 128]], compare_op=ALU.is_equal,
                                               fill=0.0, base=0, channel_multiplier=-1), reads=_b(ones), writes=_b(idb))
        P.op("pool", lambda e: e.affine_select(out=idf_t[:], in_=oa, pattern=[[1, 128]], compare_op=ALU.is_equal,
                                               fill=0.0, base=0, channel_multiplier=-1), reads=_b(ones), writes=_b(idf))

        X1 = sub(UF, UF_t[0:2, :])
        X2 = sub(KT, KT_t[0:2, :].bitcast(F32))
        X3 = sub(QT, QT_t[0:2, :].bitcast(F32))
        Hs = [sub(UB, UB_t[32 * i:32 * i + 2, :]) for i in range(4)]
        fb = sm("fb", 0, 1)
        fbv = sub(fb, sm_t[0:2, 0:1])
        nfb = sm("nfb", 1, 1)
        nfbv = sub(nfb, sm_t[0:2, 1:2])
        one2 = sm("one2", 2, 1)
        one2v = sub(one2, sm_t[0:2, 2:3])
        k_.dma(fbv, V(fb_d, []))
        k_.ts(nfbv, fbv, -1.0, ALU.mult)
        k_.memset(one2v, 1.0)
        k_.dma(X1, V(fT_d, []))
        k_.act(X1, X1, AF.Exp, scale=-1.0, bias=nfbv)
        k_.act(X1, X1, AF.Ln, bias=1.0)
        k_.ts(X1, X1, -8.0, ALU.mult)
        k_.scan(X2, sub(one2v, sm_t[0:2, 2:3].to_broadcast([2, SEQ])), X1, 0.0)
        for hh in range(2):
            k_.memset(sub(QT, QT_t[64:70, hh * SEQ:(hh + 1) * SEQ]), 1.0)
            k_.memset(sub(KT, KT_t[64:70, hh * SEQ:(hh + 1) * SEQ]), 1.0)
        cur, oth = X2, X3
        for part in range(3):
            k_.copy(Hs[0], cur)
            k_.ts(Hs[1], Hs[0], -1.0, ALU.mult)
            for hh in range(2):
                k_.dma(sub(QT, QT_t[64 + part:65 + part, hh * SEQ:(hh + 1) * SEQ]), sub(UB, UB_t[hh:hh + 1, :]))
                k_.dma(sub(KT, KT_t[67 + part:68 + part, hh * SEQ:(hh + 1) * SEQ]), sub(UB, UB_t[32 + hh:33 + hh, :]))
            if part < 2:
                k_.tt(oth, cur, Hs[0], ALU.subtract)
                cur, oth = oth, cur
        for hh in range(2):
            for c4 in range(4):
                sl = slice(c4 * 2048, (c4 + 1) * 2048)
                k_.dma(sub(QT, QT_t[0:64, hh * SEQ + c4 * 2048:hh * SEQ + (c4 + 1) * 2048]), V(qT_d[hh * 64:(hh + 1) * 64, sl], []), queue="pool")
                k_.dma(sub(KT, KT_t[0:64, hh * SEQ + c4 * 2048:hh * SEQ + (c4 + 1) * 2048]), V(kT_d[hh * 64:(hh + 1) * 64, sl], []), queue="pool")
        for c4 in range(4):
            sl = slice(c4 * 2048, (c4 + 1) * 2048)
            k_.dma(sub(UB, UB_t[:, sl]), V(uT_d[:, sl], []), queue="pool")
            k_.dma(sub(UF, UF_t[:, sl]), V(uT_d[:, sl], []))
        vv = v_d.rearrange("(s p) (h d) -> p s h d", p=128, h=2)
        k_.memset(sub(VA, VA_t[:, :, :, 64:65]), 1.0)
        for s8 in range(8):
            for hh in range(2):
                k_.dma(sub(VA, VA_t[:, s8 * 8:(s8 + 1) * 8, hh, 0:64]), V(vv[:, s8 * 8:(s8 + 1) * 8, hh, :], []), queue="pool")

        prm = V(prm_t[:], P.buf())
        k_.dma(sub(prm, prm_t[:, 0]), V(bre_d, []))
        k_.dma(sub(prm, prm_t[:, 1]), V(bim_d, []))
        k_.dma(sub(prm, prm_t[:, 2]), V(cre_d, []))
        k_.dma(sub(prm, prm_t[:, 3]), V(cim_d, []))
        A_re = sm("A_re", 4, 4); A_im = sm("A_im", 8, 4); LDT = sm("LDT", 12, 4); dsk = sm("dsk", 3, 1)
        k_.dma(A_re, V(are_d, [])); k_.dma(A_im, V(aim_d, [])); k_.dma(LDT, V(ldt_d, [])); k_.dma(dsk, V(dsk_d, []))
        are = sm("are", 16, 4); dtv = sm("dtv", 20, 4); ar = sm("ar", 24, 4); th = sm("th", 28, 4); r = sm("r", 32, 4)
        sn = sm("sn", 36, 4); cs_ = sm("cs", 40, 4); nre = sm("nre", 44, 4); nim = sm("nim", 48, 4); den = sm("den", 52, 4)
        t4a = sm("t4a", 56, 4); t4b = sm("t4b", 60, 4); cfre = sm("cfre", 64, 4); cfim = sm("cfim", 68, 4); ncfim = sm("ncfim", 72, 4)
        y4 = sm("y4", 76, 4); k4 = sm("k4", 80, 4)
        ii = V(ii_t[:], P.buf())

        def sincos(x, n, s_out, c_out, y, kf):
            iv = sub(ii, ii_t[:, 0:n])
            k_.ts(y, x, INV_2PI, ALU.mult)
            k_.copy(iv, y)
            k_.copy(kf, iv)
            k_.tt(kf, y, kf, ALU.subtract)
            k_.act(s_out, kf, AF.Sin, scale=TWO_PI_LO)
            k_.ts(y, y, 0.25, ALU.add)
            k_.copy(iv, y)
            k_.copy(kf, iv)
            k_.tt(kf, y, kf, ALU.subtract)
            k_.act(c_out, kf, AF.Sin, scale=TWO_PI_LO)

        k_.ts(are, A_re, -1e-4, ALU.min)
        k_.act(dtv, LDT, AF.Exp)
        k_.tt(ar, are, dtv, ALU.mult)
        k_.tt(th, A_im, dtv, ALU.mult)
        k_.act(r, ar, AF.Exp)
        sincos(th, 4, sn, cs_, y4, k4)
        k_.tt(nre, r, cs_, ALU.mult)
        k_.ts(nre, nre, -1.0, ALU.add)
        k_.tt(nim, r, sn, ALU.mult)
        k_.tt(den, are, are, ALU.mult)
        k_.tt(t4a, A_im, A_im, ALU.mult)
        k_.tt(den, den, t4a, ALU.add)
        k_.recip(den, den)
        k_.tt(t4a, nre, are, ALU.mult)
        k_.tt(t4b, nim, A_im, ALU.mult)
        k_.tt(t4a, t4a, t4b, ALU.add)
        k_.tt(cfre, t4a, den, ALU.mult)
        k_.tt(t4a, nim, are, ALU.mult)
        k_.tt(t4b, nre, A_im, ALU.mult)
        k_.tt(t4a, t4a, t4b, ALU.subtract)
        k_.tt(cfim, t4a, den, ALU.mult)
        k_.ts(ncfim, cfim, -1.0, ALU.mult)

        Wv = [V(W_t[:, i, :], P.buf()) for i in range(6)]
        iot = V(yst_t[:].rearrange("p a b -> p (a b)")[:, 0:TS + 1], P.buf())
        P.op("pool", lambda e: e.iota(ii_t[:], pattern=[[1, TS + 1]], base=0, channel_multiplier=0), writes=_b(ii))
        k_.copy(iot, ii)
        cosT = [V(cos_t[:, i, :], P.buf()) for i in range(4)]
        sinT = [V(sin_t[:, i, :], P.buf()) for i in range(4)]
        E0 = V(E_t[:, 0, :], P.buf())
        E1 = V(E_t[:, 1, :], P.buf())
        LB_re = [V(L_t[:, i, :], P.buf()) for i in range(4)]
        LB_im = [V(L_t[:, 4 + i, :], P.buf()) for i in range(4)]
        LC_re = [V(L_t[:, 8 + i, :], P.buf()) for i in range(4)]
        LC_im = [V(L_t[:, 12 + i, :], P.buf()) for i in range(4)]
        bb = V(bb_t[:], P.buf())
        angs = V(UF_t[:, 0:2 * (TS + 1)].rearrange("p (a n) -> p a n", a=2), UF.bufs)
        ang_t = sb("ang", [128, 3, TS + 1], F32)
        ang = [V(ang_t[:, i, :], P.buf()) for i in range(3)]
        for i in range(4):
            thi = sub(th, th.ap[:, i:i + 1])
            k_.ts(ang[0], iot, thi, ALU.mult)
            sincos(ang[0], TS + 1, sinT[i], cosT[i], ang[1], ang[2])
            cr = sub(cfre, cfre.ap[:, i:i + 1]); ci = sub(cfim, cfim.ap[:, i:i + 1]); nci = sub(ncfim, ncfim.ap[:, i:i + 1])
            bre_i = sub(prm, prm_t[:, 0, i, :]); bim_i = sub(prm, prm_t[:, 1, i, :])
            cre_i = sub(prm, prm_t[:, 2, i, :]); cim_i = sub(prm, prm_t[:, 3, i, :])
            Bre = sub(bb, bb_t[:, 0, i, :]); Bim = sub(bb, bb_t[:, 1, i, :]); tb = sub(bb, bb_t[:, 2, i, :])
            k_.ts(tb, bre_i, cr, ALU.mult)
            k_.stt(Bre, bim_i, nci, tb, ALU.mult, ALU.add)
            k_.ts(tb, bim_i, cr, ALU.mult)
            k_.stt(Bim, bre_i, ci, tb, ALU.mult, ALU.add)
            for (src, dstL, neg, tr) in ((Bre, LB_re[i], False, True), (Bim, LB_im[i], False, True),
                                         (cre_i, LC_re[i], False, False), (cim_i, LC_im[i], True, False)):
                k_.copy(E0, zeros)
                for a in range(2):
                    gl = 2 * i + a
                    dst = sub(E0, E_t[a * 64:(a + 1) * 64, 0, gl * 16:(gl + 1) * 16])
                    s_ = sub(src, src.ap[a * 64:(a + 1) * 64, :])
                    if neg:
                        k_.ts(dst, s_, -1.0, ALU.mult)
                    else:
                        k_.copy(dst, s_)
                if tr:
                    tp_ = S_ps.get()
                    k_.transpose(sub(tp_, tp_.ap[:, 0:128]), E0, idf)
                    k_.copy(dstL, sub(tp_, tp_.ap[:, 0:128]))
                else:
                    k_.copy(dstL, E0)

        init_re = [sm("ire%d" % i, 84 + i, 1) for i in range(4)]
        init_im = [sm("iim%d" % i, 88 + i, 1) for i in range(4)]
        cc = [sm("cc%d" % i, 92 + i, 1) for i in range(4)]
        for i in range(4):
            k_.memset(init_re[i], 0.0)
            k_.memset(init_im[i], 0.0)
        Zre = Rot(P, [Z_t[:, i, :] for i in range(2)])
        Zim = Rot(P, [Z_t[:, 2 + i, :] for i in range(2)])
        yst = Rot(P, [yst_t[:, i, :] for i in range(2)])

        def ssm_gen():
            pending = None
            for c in range(NCH):
                cols = slice(c * TS, (c + 1) * TS)
                for i in range(4):
                    ub = sub(UB, UB_t[:, cols])
                    k_.mm(BUre, LB_re[i], ub, True, True)
                    k_.mm(BUim, LB_im[i], ub, True, True)
                    co = sub(cosT[i], cosT[i].ap[:, 0:TS]); si = sub(sinT[i], sinT[i].ap[:, 0:TS])
                    W = Wv
                    k_.tt(W[0], BUre, co, ALU.mult)
                    k_.tt(W[1], BUim, si, ALU.mult)
                    k_.tt(W[0], W[0], W[1], ALU.add)
                    k_.tt(W[1], BUim, co, ALU.mult)
                    k_.tt(W[2], BUre, si, ALU.mult)
                    k_.tt(W[1], W[1], W[2], ALU.subtract)
                    rb = sub(r, r.ap[:, i:i + 1].to_broadcast([128, TS]))
                    k_.scan(W[3], rb, W[0], init_re[i])
                    k_.scan(W[4], rb, W[1], init_im[i])
                    cT_ = sub(cosT[i], cosT[i].ap[:, TS:TS + 1]); sT_ = sub(sinT[i], sinT[i].ap[:, TS:TS + 1])
                    lre = sub(W[3], W[3].ap[:, TS - 1:TS]); lim = sub(W[4], W[4].ap[:, TS - 1:TS])
                    k_.tt(cc[0], lre, cT_, ALU.mult)
                    k_.tt(cc[1], lim, sT_, ALU.mult)
                    k_.tt(cc[2], lre, sT_, ALU.mult)
                    k_.tt(cc[3], lim, cT_, ALU.mult)
                    k_.tt(init_re[i], cc[0], cc[1], ALU.subtract)
                    k_.tt(init_im[i], cc[2], cc[3], ALU.add)
                    zr = Zre.get(); zi = Zim.get()
                    k_.tt(W[0], W[3], co, ALU.mult)
                    k_.tt(W[1], W[4], si, ALU.mult)
                    k_.tt(zr, W[0], W[1], ALU.subtract)
                    k_.tt(W[0], W[3], si, ALU.mult)
                    k_.tt(W[1], W[4], co, ALU.mult)
                    k_.tt(zi, W[0], W[1], ALU.add)
                    yield
                    if pending is not None:
                        pending()
                    def fin(c=c, i=i, zr=zr, zi=zi, cols=cols):
                        k_.mm(Yps, LC_re[i], zr, (i == 0), False)
                        k_.mm(Yps, LC_im[i], zi, False, (i == 3))
                        if i == 3:
                            ys = yst.get()
                            k_.stt(ys, sub(UF, UF_t[:, cols]), dsk, Yps, ALU.mult, ALU.add)
                            k_.dma(V(ys_o[:, cols], []), ys, is_out=True)
                    pending = fin
            pending()

        PT = Rot(P, [PT_t[:, i, :] for i in range(4)])
        ost = Rot(P, [ost_t[:, i] for i in range(2)])
        rcs = Rot(P, [rc_t[:, i:i + 1] for i in range(8)])

        def attn_gen():
            units = [(hh, qb, kb) for hh in range(2) for qb in range(16) for kb in range(4 * qb + 4)]
            st_ = {}

            def qk(n):
                hh, qb, kb = units[n]
                c0 = 0 if kb < 4 * qb else (kb - 4 * qb) * 128
                S = S_ps.get()
                diag = kb >= 4 * qb
                k_.mm(sub(S, S.ap[:, c0:512]), sub(KT, KT_t[0:70, hh * SEQ + kb * 128:hh * SEQ + (kb + 1) * 128]),
                      sub(QT, QT_t[0:70, hh * SEQ + qb * 512 + c0:hh * SEQ + (qb + 1) * 512]), True, not diag)
                if diag:
                    k_.mm(sub(S, S.ap[:, c0:c0 + 128]), idb, msk, False, True)
                pt = PT.get()
                k_.act(sub(pt, pt.ap[:, c0:512]), sub(S, S.ap[:, c0:512]), AF.Exp, scale=0.125)
                st_[n] = (pt, c0)

            def pv(n):
                hh, qb, kb = units[n]
                pt, c0 = st_.pop(n)
                if kb == 0:
                    st_["O"] = O_ps.get()
                O = st_["O"]
                for tq in range(c0 // 128, 4):
                    k_.mm(sub(O, O.ap[:, tq * 65:(tq + 1) * 65]), sub(pt, pt.ap[:, tq * 128:(tq + 1) * 128]),
                          sub(VA, VA_t[:, kb, hh, :]), (kb == 0 and tq == 0), (kb == 4 * qb + tq))
                if kb == 4 * qb + 3:
                    os_ = ost.get()
                    for tq in range(4):
                        rc = rcs.get()
                        k_.recip(rc, sub(O, O.ap[:, tq * 65 + 64:tq * 65 + 65]))
                        k_.act(sub(os_, os_.ap[:, tq, :]), sub(O, O.ap[:, tq * 65:tq * 65 + 64]), AF.Identity, scale=rc)
                    dst = at_o[qb * 512:(qb + 1) * 512, hh * 64:(hh + 1) * 64].rearrange("(t p) d -> p t d", p=128)
                    k_.dma(V(dst, []), os_, is_out=True)

            LOOK = 2
            N = len(units)
            for n in range(min(LOOK, N)):
                qk(n)
            for n in range(N):
                if n + LOOK < N:
                    qk(n + LOOK)
                pv(n)
                yield

        ga = attn_gen()
        gs_ = ssm_gen()
        n_a = 2 * sum(4 * qb + 4 for qb in range(16))
        n_s = NCH * 4
        per = max(1, n_a // n_s)
        done_a = done_s = False
        cnt = 0
        while not (done_a and done_s):
            if not done_s and (cnt % per == 0 or done_a):
                try:
                    next(gs_)
                except StopIteration:
                    done_s = True
            if not done_a:
                try:
                    next(ga)
                except StopIteration:
                    done_a = True
            cnt += 1
        P.emit()
    return nc


def mix_inputs(inp, l, j, uT_full, qT_full, kT_full, vtok_full, fT_full):
    g0 = 8 * j
    f32 = np.float32

    def pair_cols(a):
        return np.ascontiguousarray(a.reshape(4, 2, 64).transpose(1, 2, 0).reshape(128, 4), dtype=f32)

    m = {
        "uT": np.ascontiguousarray(uT_full[128 * j:128 * (j + 1)]),
        "qT": np.ascontiguousarray(qT_full[128 * j:128 * (j + 1)]),
        "kT": np.ascontiguousarray(kT_full[128 * j:128 * (j + 1)]),
        "vtok": np.ascontiguousarray(vtok_full[:, 128 * j:128 * (j + 1)]),
        "fT": np.ascontiguousarray(fT_full[2 * j:2 * j + 2]),
        "fb": np.ascontiguousarray(inp["forget_b"][l, 2 * j:2 * j + 2].reshape(2, 1), dtype=f32),
        "a_re": pair_cols(inp["ssm_a_re"][l, g0:g0 + 8]),
        "a_im": pair_cols(inp["ssm_a_im"][l, g0:g0 + 8]),
        "ldt": pair_cols(np.repeat(inp["ssm_log_dt"][l, g0:g0 + 8][:, None], 64, axis=1)),
        "b_re": np.ascontiguousarray(inp["ssm_b_re"][l, g0:g0 + 8].reshape(4, 2, 64, 16).transpose(1, 2, 0, 3).reshape(128, 4, 16), dtype=f32),
        "b_im": np.ascontiguousarray(inp["ssm_b_im"][l, g0:g0 + 8].reshape(4, 2, 64, 16).transpose(1, 2, 0, 3).reshape(128, 4, 16), dtype=f32),
        "c_re": np.ascontiguousarray(inp["ssm_c_re"][l, g0:g0 + 8].reshape(4, 2, 16, 64).transpose(1, 3, 0, 2).reshape(128, 4, 16), dtype=f32),
        "c_im": np.ascontiguousarray(inp["ssm_c_im"][l, g0:g0 + 8].reshape(4, 2, 16, 64).transpose(1, 3, 0, 2).reshape(128, 4, 16), dtype=f32),
        "dskip": np.ascontiguousarray(inp["ssm_d"][l, 128 * j:128 * (j + 1)].reshape(128, 1), dtype=f32),
    }
    return m


def _run(nc, maps):
    res = run_bass_kernel_spmd(nc, maps, core_ids=list(range(NCORES)))
    return res.results


def kernel(**inputs):
    inp = {k: np.asarray(v) for k, v in inputs.items()}
    x = inp["x"].astype(np.float32, copy=False)
    cores = [(b, j) for b in range(BATCH) for j in range(4)]
    cT = [_colT(inp["c"][b]) for b in range(BATCH)]

    def tp_launch(stage, xin_list, extra_list):
        common = tp_common_inputs(inp, stage)
        maps = []
        for ci, (b, j) in enumerate(cores):
            m = dict(common)
            m["xin"] = xin_list[ci]
            m["cT"] = cT[b]
            m.update(extra_list[ci])
            maps.append(m)
        return _run(get_nc(("tp", stage), lambda: build_tp(stage)), maps)

    def mix_launch(l, tp_res):
        maps = []
        for b in range(BATCH):
            cat = lambda name, ax: np.concatenate([tp_res[b * 4 + j][name] for j in range(4)], axis=ax)
            uT, qT, kT, fT = cat("uT", 1), cat("qT", 1), cat("kT", 1), cat("fT", 1)
            vt = cat("vtok", 0)
            for j in range(4):
                maps.append(mix_inputs(inp, l, j, uT, qT, kT, vt, fT))
        r = _run(get_nc(("mix",), build_mix), maps)
        extra = []
        for b in range(BATCH):
            ys = np.concatenate([r[b * 4 + j]["yssmT"] for j in range(4)], axis=0)
            atT = np.concatenate([r[b * 4 + j]["attn"] for j in range(4)], axis=1).T
            for j in range(4):
                sl = slice(j * TOK, (j + 1) * TOK)
                extra.append({"yssmT": np.ascontiguousarray(ys[:, sl]), "attnT": np.ascontiguousarray(atT[:, sl]),
                              "sgaT_in": tp_res[b * 4 + j]["sgaT"], "sgbT_in": tp_res[b * 4 + j]["sgbT"]})
        return extra

    xin = [np.ascontiguousarray(x[b, j * TOK:(j + 1) * TOK].T) for (b, j) in cores]
    r = tp_launch("A0", xin, [{} for _ in cores])
    extra = mix_launch(0, r)
    r = tp_launch("CA", [r[ci]["xout"] for ci in range(NCORES)], extra)
    extra = mix_launch(1, r)
    r = tp_launch("C1", [r[ci]["xout"] for ci in range(NCORES)], extra)
    out = np.empty((BATCH, SEQ, D_MODEL), np.float32)
    for ci, (b, j) in enumerate(cores):
        out[b, j * TOK:(j + 1) * TOK] = r[ci]["xout"].T
    return out
```
